# Optimizing a Trainium2 kernel written in Bass

```python
import jax, jax.numpy as jnp
from jax import lax
import numpy as np

D_MODEL = 2048
BATCH = 1
SEQ = 16384
DEPTH = 4
DEC_BATCH = 1
DEC_SEQ = 8192
PAST_LEN = 128

D_R = 2048
H_R = 16
BR = D_R // H_R
CONV_W = 4
CONV_LEFT = 2
RG_C = 8.0
D_H = 2048
H_H = 16
DK = D_H // H_H
DV = D_H // H_H
CHUNK = 64
IN_SPLITS = (D_R, D_R, D_H, D_H, D_H, D_H, D_H, D_MODEL, D_MODEL)
IN_COLS = 2 * D_R + 5 * D_H + 2 * D_MODEL
EPS = 1e-6

kernel_name = "hawk_hgrn2_bidir_gated_parallel_encoder"


def rmsnorm(x, gain, eps=EPS):
    xf = x.astype(jnp.float32)
    y = xf * lax.rsqrt(jnp.mean(xf * xf, axis=-1, keepdims=True) + eps)
    return (y * gain.astype(jnp.float32)).astype(x.dtype)


def centred_depthwise_conv(x, w, b):
    s = x.shape[1]
    xp = jnp.pad(x, ((0, 0), (CONV_LEFT, CONV_W - 1 - CONV_LEFT), (0, 0)))
    out = b + xp[:, 0:s] * w[0]
    for j in range(1, CONV_W):
        out = out + xp[:, j:j + s] * w[j]
    return out


def _lin_combine(left, right):
    a1, b1 = left
    a2, b2 = right
    return a1 * a2, a2 * b1 + b2


def rglru_direction(xf, wa, ba, wx, bx, lam, reverse):
    bsz, s, _ = xf.shape
    xb = xf.reshape(bsz, s, H_R, BR)
    r = jax.nn.sigmoid(jnp.einsum('bshi,hij->bshj', xb, wa.astype(jnp.float32)).reshape(bsz, s, D_R) + ba.astype(jnp.float32))
    i = jax.nn.sigmoid(jnp.einsum('bshi,hij->bshj', xb, wx.astype(jnp.float32)).reshape(bsz, s, D_R) + bx.astype(jnp.float32))
    log_a = -RG_C * r * jax.nn.softplus(-lam.astype(jnp.float32))
    a = jnp.exp(log_a)
    bval = jnp.sqrt(-jnp.expm1(2.0 * log_a)) * (i * xf)
    _, h = lax.associative_scan(_lin_combine, (a, bval), axis=1, reverse=reverse)
    return h


def hgrn2_forward_scan(q, k, v, logf):
    bsz, s = q.shape[0], q.shape[1]
    nc = s // CHUNK

    def to_chunks(t):
        return t.reshape(bsz, nc, CHUNK, H_H, t.shape[-1]).transpose(1, 0, 3, 2, 4)

    mask = jnp.tril(jnp.ones((CHUNK, CHUNK), dtype=bool))[:, :, None]

    def body(state, inp):
        qc, kc, vc, gc = inp
        bcum = jnp.cumsum(gc, axis=-2)
        o_inter = jnp.einsum('bhtk,bhkv->bhtv', qc * jnp.exp(bcum), state)
        diff = bcum[:, :, :, None, :] - bcum[:, :, None, :, :]
        decay = jnp.exp(jnp.where(mask, diff, -jnp.inf))
        scores = jnp.einsum('bhtk,bhsk,bhtsk->bhts', qc, kc, decay)
        o_intra = jnp.einsum('bhts,bhsv->bhtv', scores, vc)
        blast = bcum[:, :, -1:, :]
        kd = kc * jnp.exp(blast - bcum)
        new_state = jnp.exp(blast[:, :, 0, :])[..., None] * state + jnp.einsum('bhsk,bhsv->bhkv', kd, vc)
        return new_state, o_inter + o_intra

    init = jnp.zeros((bsz, H_H, DK, DV), jnp.float32)
    _, o = lax.scan(body, init, (to_chunks(q), to_chunks(k), to_chunks(v), to_chunks(logf)))
    return o.transpose(1, 0, 3, 2, 4).reshape(bsz, s, H_H, DV)


def hgrn2_gates(z, lb):
    logf = jnp.logaddexp(jnp.log(lb), jnp.log1p(-lb) + jax.nn.log_sigmoid(z))
    k = (1.0 - lb) * jax.nn.sigmoid(-z)
    return logf, k


def trunk(x, norm_gain, w_in, conv_w, conv_b, rg_wa, rg_ba, rg_wx, rg_bx, rg_lambda,
          hg_lower, hg_norm_gain, w_down_r, w_down_h, w_out, final_gain):
    bsz, s, _ = x.shape
    lb_all = jnp.cumsum(jax.nn.softmax(hg_lower.astype(jnp.float32), axis=1), axis=1)
    lb_all = lb_all - lb_all[:, :1]
    split_idx = list(np.cumsum(IN_SPLITS)[:-1])
    for l in range(DEPTH):
        h = rmsnorm(x, norm_gain[l])
        proj = h @ w_in[l]
        xr, gr, qh, zf, zb, vh, gh, mr, mh = jnp.split(proj, split_idx, axis=-1)

        xc = centred_depthwise_conv(xr, conv_w[l], conv_b[l]).astype(jnp.float32)
        hr = (rglru_direction(xc, rg_wa[l, 0], rg_ba[l, 0], rg_wx[l, 0], rg_bx[l, 0], rg_lambda[l, 0], False)
              + rglru_direction(xc, rg_wa[l, 1], rg_ba[l, 1], rg_wx[l, 1], rg_bx[l, 1], rg_lambda[l, 1], True))
        yr = (hr * jax.nn.silu(gr.astype(jnp.float32))).astype(x.dtype) @ w_down_r[l]

        q4 = jax.nn.silu(qh.astype(jnp.float32)).reshape(bsz, s, H_H, DK)
        v4 = vh.astype(jnp.float32).reshape(bsz, s, H_H, DV)
        logf_f, k_f = hgrn2_gates(zf.astype(jnp.float32), lb_all[0, l])
        logf_b, k_b = hgrn2_gates(zb.astype(jnp.float32), lb_all[1, l])
        o_f = hgrn2_forward_scan(q4, k_f.reshape(bsz, s, H_H, DK), v4, logf_f.reshape(bsz, s, H_H, DK))
        o_b = jnp.flip(hgrn2_forward_scan(jnp.flip(q4, 1), jnp.flip(k_b.reshape(bsz, s, H_H, DK), 1),
                                          jnp.flip(v4, 1), jnp.flip(logf_b.reshape(bsz, s, H_H, DK), 1)), 1)
        o = rmsnorm(o_f + o_b, hg_norm_gain[l].reshape(H_H, DV)).reshape(bsz, s, D_H)
        yh = (o * jax.nn.silu(gh.astype(jnp.float32))).astype(x.dtype) @ w_down_h[l]

        merged = jax.nn.sigmoid(mr) * yr + jax.nn.sigmoid(mh) * yh
        x = x + merged @ w_out[l]
    return rmsnorm(x, final_gain)


def setup_inputs(seed: int = 0) -> dict:
    key = jax.random.key(seed)
    ks = jax.random.split(key, 18)
    f32 = jnp.float32
    x_prompt = jax.random.normal(ks[0], (BATCH, SEQ, D_MODEL), f32)
    x_sample = jax.random.normal(ks[1], (DEC_BATCH, DEC_SEQ, D_MODEL), f32)
    norm_gain = 1.0 + 0.02 * jax.random.normal(ks[2], (DEPTH, D_MODEL), f32)
    w_in = jax.random.normal(ks[3], (DEPTH, D_MODEL, IN_COLS), f32) * D_MODEL ** -0.5
    conv_w = jax.random.normal(ks[4], (DEPTH, CONV_W, D_R), f32) * CONV_W ** -0.5
    conv_b = 0.02 * jax.random.normal(ks[5], (DEPTH, D_R), f32)
    rg_wa = jax.random.normal(ks[6], (DEPTH, 2, H_R, BR, BR), f32) * BR ** -0.5
    rg_ba = 0.02 * jax.random.normal(ks[7], (DEPTH, 2, D_R), f32)
    rg_wx = jax.random.normal(ks[8], (DEPTH, 2, H_R, BR, BR), f32) * BR ** -0.5
    rg_bx = 0.02 * jax.random.normal(ks[9], (DEPTH, 2, D_R), f32)
    a0 = jax.random.uniform(ks[10], (DEPTH, 2, D_R), f32, 0.9, 0.999)
    p = a0 ** (1.0 / RG_C)
    rg_lambda = jnp.log(p) - jnp.log1p(-p)
    hg_lower = 0.1 * jax.random.normal(ks[11], (2, DEPTH, D_H), f32)
    hg_norm_gain = 1.0 + 0.02 * jax.random.normal(ks[12], (DEPTH, D_H), f32)
    w_down_r = jax.random.normal(ks[13], (DEPTH, D_R, D_MODEL), f32) * D_R ** -0.5
    w_down_h = jax.random.normal(ks[14], (DEPTH, D_H, D_MODEL), f32) * D_H ** -0.5
    w_out = jax.random.normal(ks[15], (DEPTH, D_MODEL, D_MODEL), f32) * (D_MODEL * 2 * DEPTH) ** -0.5
    final_gain = 1.0 + 0.02 * jax.random.normal(ks[16], (D_MODEL,), f32)
    return {"x_prompt": x_prompt, "x_sample": x_sample, "norm_gain": norm_gain, "w_in": w_in,
            "conv_w": conv_w, "conv_b": conv_b, "rg_wa": rg_wa, "rg_ba": rg_ba, "rg_wx": rg_wx,
            "rg_bx": rg_bx, "rg_lambda": rg_lambda, "hg_lower": hg_lower, "hg_norm_gain": hg_norm_gain,
            "w_down_r": w_down_r, "w_down_h": w_down_h, "w_out": w_out, "final_gain": final_gain}


def reference(x_prompt, x_sample, norm_gain, w_in, conv_w, conv_b, rg_wa, rg_ba, rg_wx, rg_bx,
              rg_lambda, hg_lower, hg_norm_gain, w_down_r, w_down_h, w_out, final_gain):
    y_prompt = trunk(x_prompt, norm_gain, w_in, conv_w, conv_b, rg_wa, rg_ba, rg_wx, rg_bx, rg_lambda,
                     hg_lower, hg_norm_gain, w_down_r, w_down_h, w_out, final_gain)
    y_sample = trunk(x_sample, norm_gain, w_in, conv_w, conv_b, rg_wa, rg_ba, rg_wx, rg_bx, rg_lambda,
                     hg_lower, hg_norm_gain, w_down_r, w_down_h, w_out, final_gain)
    return (y_prompt, y_sample)
```

```python
import numpy as np
from contextlib import ExitStack
import concourse.bass as bass
import concourse.mybir as mybir
from concourse.bass_utils import run_bass_kernel_spmd

F32 = mybir.dt.float32
BF16 = mybir.dt.bfloat16
ALU = mybir.AluOpType
AF = mybir.ActivationFunctionType

D = 2048
NH = 16
NCORE = 8
NSEG = 3
CH = 64
EPS = 1e-6
NCV = 48
PAY = 64 + 32 + 2 * 2048
O_HF, O_AF, O_HB, O_AB, O_DF, O_DB, O_SF, O_SB = 0, 16, 32, 48, 64, 80, 96, 96 + 2048


class Cnt:
    def __init__(self, sem):
        self.sem = sem
        self.count = 0


class Buf:
    __slots__ = ("w", "r")

    def __init__(self):
        self.w = None
        self.r = {}


class T:
    def __init__(self, t):
        self.t = t
        self.b = Buf()

    def __getitem__(self, k):
        return self.t[k]


class V:
    def __init__(self, ap):
        self.ap = ap
        self.b = Buf()

    def __getitem__(self, k):
        return self.ap[k]


class Sched:
    def __init__(self, nc, es, ndma=10):
        self.nc = nc
        mk = lambda n: es.enter_context(nc.semaphore(n))
        self.eng = {"pe": nc.tensor, "act": nc.scalar, "dve": nc.vector, "pool": nc.gpsimd, "sp": nc.sync}
        self.cnt = {k: Cnt(mk("s_" + k)) for k in self.eng}
        self.waited = {k: {} for k in self.eng}
        self.rings = {q: [Cnt(mk(f"d_{q}{i}")) for i in range(ndma)] for q in ("sp", "pool")}
        self.rpos = {"sp": 0, "pool": 0}
        self.cc = Cnt(mk("s_cc"))
        self.n = 0

    def _wait(self, en, c, n):
        w = self.waited[en]
        if w.get(c, 0) >= n:
            return
        if c is self.cnt.get(en):
            if en == "pe" or n > c.count:
                return
        self.eng[en].wait_ge(c.sem, n)
        w[c] = n
        self.n += 1

    def _deps(self, en, reads, writes):
        need = {}
        for b in reads:
            b = getattr(b, 'b', b)
            if b.w is not None:
                c, n = b.w
                need[c] = max(need.get(c, 0), n)
        for b in writes:
            b = getattr(b, 'b', b)
            if b.w is not None:
                c, n = b.w
                need[c] = max(need.get(c, 0), n)
            for c, n in b.r.items():
                need[c] = max(need.get(c, 0), n)
        for c, n in need.items():
            self._wait(en, c, n)

    def _mark(self, c, n, reads, writes):
        for b in reads:
            b = getattr(b, 'b', b)
            if b.r.get(c, 0) < n:
                b.r[c] = n
        for b in writes:
            b = getattr(b, 'b', b)
            b.w = (c, n)
            b.r = {}

    def op(self, en, fn, reads=(), writes=(), signal=True):
        self._deps(en, reads, writes)
        ins = fn(self.eng[en])
        self.n += 1
        c = self.cnt[en]
        if signal:
            c.count += 1
            ins.then_inc(c.sem, 1)
            n = c.count
        else:
            n = c.count + 1
        self._mark(c, n, reads, writes)
        return ins

    def dma(self, q, out, in_, reads=(), writes=()):
        ring = self.rings[q]
        c = ring[self.rpos[q] % len(ring)]
        self.rpos[q] += 1
        if c.count:
            self._wait(q, c, c.count)
        self._deps(q, reads, writes)
        self.eng[q].dma_start(out=out, in_=in_).then_inc(c.sem, 16)
        self.n += 1
        c.count += 16
        self._mark(c, c.count, reads, writes)

    def collective(self, ins, outs, reads, writes):
        self._deps("pool", reads, writes)
        c = self.cc
        self.nc.gpsimd.collective_compute("AllReduce", ALU.add, replica_groups=[list(range(NCORE))],
                                          ins=ins, outs=outs).then_inc(c.sem, 1)
        c.count += 1
        self.n += 1
        self._mark(c, c.count, reads, writes)

    def barrier(self):
        cs = list(self.cnt.values()) + [c for r in self.rings.values() for c in r] + [self.cc]
        for en in self.eng:
            for c in cs:
                if c.count:
                    self._wait(en, c, c.count)

    def act(self, out, in_, func, r, w, bias=None, scale=1.0, accum=None):
        kw = {}
        if bias is not None:
            kw["bias"] = bias
        if accum is not None:
            kw["accum_out"] = accum
        return self.op("act", lambda e: e.activation(out=out, in_=in_, func=func, scale=scale, **kw), r, w)

    def tt(self, en, out, a, b, op, r, w):
        return self.op(en, lambda e: e.tensor_tensor(out=out, in0=a, in1=b, op=op), r, w)

    def ts(self, en, out, a, s1, op0, r, w, s2=None, op1=None):
        if op1 is None:
            return self.op(en, lambda e: e.tensor_scalar(out=out, in0=a, scalar1=s1, scalar2=None, op0=op0), r, w)
        return self.op(en, lambda e: e.tensor_scalar(out=out, in0=a, scalar1=s1, scalar2=s2, op0=op0, op1=op1), r, w)

    def stt(self, out, a, s, b, op0, op1, r, w):
        return self.op("dve", lambda e: e.scalar_tensor_tensor(out=out, in0=a, scalar=s, in1=b, op0=op0, op1=op1), r, w)

    def scan(self, out, d0, d1, init, r, w):
        return self.op("dve", lambda e: e.tensor_tensor_scan(out=out, data0=d0, data1=d1, initial=init,
                                                             op0=ALU.mult, op1=ALU.add), r, w)

    def copy(self, en, out, in_, r, w):
        if en == "act":
            return self.act(out, in_, AF.Copy, r, w)
        return self.op(en, lambda e: e.tensor_copy(out=out, in_=in_), r, w)

    def mm(self, out, lhsT, rhs, start, stop, r, w, signal=True):
        return self.op("pe", lambda e: e.matmul(out, lhsT, rhs, start=start, stop=stop), r, w, signal=signal)

    def tr(self, out, in_, ident, r, w, signal=True):
        return self.op("pe", lambda e: e.transpose(out, in_, ident), r, w, signal=signal)


def build(TS, DEPTH, dbg=False):
    NL = NSEG * TS
    NCH = TS // CH
    SUB = min(512, TS)
    NSUBL = NL // SUB
    NSUBS = TS // SUB
    NG = NL // 128
    GPS = SUB // 128
    nc = bass.Bass("TRN2", target_bir_lowering=False)

    def din(name, shape, dt=F32):
        return nc.dram_tensor(name, shape, dt, kind="ExternalInput")

    x_in = din("x", [NL, D])
    w_in = din("w_in", [DEPTH, 144, 128, 2048])
    w_dr = din("w_dr", [DEPTH, 16, 128, 2048])
    w_dh = din("w_dh", [DEPTH, 16, 128, 2048])
    w_out = din("w_out", [DEPTH, 4, 128, 16 * 512])
    w_g = din("w_g", [DEPTH, 16, 128, 512])
    gains = din("gains", [DEPTH + 1, D])
    pvec = din("pvec", [DEPTH, 128, 14 * 16])
    hgl = din("hgl", [128, 2 * DEPTH * 16])
    cvec = din("cvec", [128, NCV])
    ident_in = din("ident", [128, 128])
    masks_in = din("masks", [2, 64, TS])
    cmask_in = din("cmask", [2, 128, TS])
    y_out = nc.dram_tensor("y", [NL, D], F32, kind="ExternalOutput")
    if dbg:
        dbg_proj = nc.dram_tensor("dbg_proj", [144, 128, NL], F32, kind="ExternalOutput")
        dbg_u = nc.dram_tensor("dbg_u", [2, 16, 128, NL], F32, kind="ExternalOutput")
        dbg_x = nc.dram_tensor("dbg_x", [NL, D], F32, kind="ExternalOutput")

    PROJ = nc.dram_tensor("PROJ", [144, 128, NL], F32)
    HS = nc.dram_tensor("HS", [16, 128, NL], F32)
    AFd = nc.dram_tensor("AFd", [16, 128, NL], F32)
    ABd = nc.dram_tensor("ABd", [16, 128, NL], F32)
    OS = nc.dram_tensor("OS", [16, 128, NL], F32)
    QGF = nc.dram_tensor("QGF", [16, 128, NL], BF16)
    QGB = nc.dram_tensor("QGB", [16, 128, NL], BF16)
    XS = nc.dram_tensor("XS", [NL, D], F32)
    ZH = nc.dram_tensor("ZH", [128, 8 * 48], F32)
    RH = nc.dram_tensor("RH", [128, 8 * 48], F32)
    Z = nc.dram_tensor("Z", [8 * 128, PAY], F32)
    R = nc.dram_tensor("R", [8 * 128, PAY], F32)
    bPROJ = [Buf() for _ in range(144)]
    bHS = [Buf() for _ in range(16)]; bAF = [Buf() for _ in range(16)]; bAB = [Buf() for _ in range(16)]
    bOS = [Buf() for _ in range(16)]; bQGF = [Buf() for _ in range(16)]; bQGB = [Buf() for _ in range(16)]
    bXS = [Buf() for _ in range(NG)]
    bZH, bRH, bZ, bR = Buf(), Buf(), Buf(), Buf()

    with ExitStack() as es:
        uid = [0]

        def sb(name, shape, dt=F32, st=es):
            uid[0] += 1
            return T(st.enter_context(nc.sbuf_tensor(f"sb{uid[0]}_{name}", shape, dt)))

        def ps(name, shape, dt=F32):
            return T(es.enter_context(nc.psum_tensor("ps_" + name, shape, dt)))

        S = Sched(nc, es)
        es.enter_context(nc.Block())

        pT = ps("pT", [128, 2048], BF16)
        pA = [ps("pA0", [128, 512]), ps("pA1", [128, 512])]
        pB = [ps("pB0", [128, 512]), ps("pB1", [128, 512])]
        pU = ps("pU", [128, 4, 128])
        pX = ps("pX", [128, 512])
        acc4 = pA + pB

        ident = sb("ident", [128, 128], BF16)
        identf = sb("identf", [128, 128])
        ones_bf = sb("ones_bf", [128, 128], BF16)
        cv = sb("cv", [128, NCV])
        pv = sb("pv", [128, 14 * 16])
        pd = sb("pd", [128, 8 * 16])
        lbt = sb("lbt", [128, 2 * DEPTH * 16])
        hl = sb("hl", [128, 2 * DEPTH * 16])
        halo = sb("halo", [128, 16, 3])
        tiny = sb("tiny", [128, 64])
        LS = {}

        S.dma("sp", identf[:, :], ident_in[:, :], [], [identf])
        S.copy("dve", ident[:, :], identf[:, :], [identf], [ident])
        S.op("dve", lambda e: e.memset(ones_bf[:, :], 1.0), [], [ones_bf])
        S.dma("sp", cv[:, :], cvec[:, :], [], [cv])
        S.dma("sp", hl[:, :], hgl[:, :], [], [hl])
        with ExitStack() as e0:
            ex = sb("ex", [128, 2 * DEPTH * 16], F32, e0)
            sm = sb("sm", [128, 2 * 16], F32, e0)
            S.act(ex[:, :], hl[:, :], AF.Exp, [hl], [ex])
            ex4 = ex[:, :].rearrange("p (d l h) -> p d l h", d=2, l=DEPTH)
            sm3 = sm[:, :].rearrange("p (d h) -> p d h", d=2)
            lb4 = lbt[:, :].rearrange("p (d l h) -> p d l h", d=2, l=DEPTH)
            S.copy("dve", sm3, ex4[:, :, 0, :], [ex], [sm])
            for l in range(1, DEPTH):
                S.tt("dve", sm3, sm3, ex4[:, :, l, :], ALU.add, [ex, sm], [sm])
            S.op("dve", lambda e: e.reciprocal(out=sm[:, :], in_=sm[:, :]), [sm], [sm])
            S.op("dve", lambda e: e.memset(lbt[:, :], 0.0), [], [lbt])
            for l in range(1, DEPTH):
                S.tt("dve", lb4[:, :, l, :], ex4[:, :, l, :], sm3, ALU.mult, [ex, sm, lbt], [lbt])
                S.tt("dve", lb4[:, :, l, :], lb4[:, :, l, :], lb4[:, :, l - 1, :], ALU.add, [lbt], [lbt])
            S.barrier()

        P = lambda i, hd: pv[:, i * 16 + hd:i * 16 + hd + 1]
        PD = lambda i, hd: pd[:, i * 16 + hd:i * 16 + hd + 1]
        CV = lambda i: cv[:, i:i + 1]
        C_OH, C_GF, C_GB, C_SL, C_SR, C_RS, C_NOM = 0, 8, 16, 24, 32, 40, 41

        def layer_params(l):
            S.dma("sp", pv[:, :], pvec[l], [], [pv])
            for d in range(2):
                src = pv[:, (7 + 3 * d) * 16:(8 + 3 * d) * 16]
                dst = pd[:, d * 16:(d + 1) * 16]
                S.act(dst, src, AF.Exp, [pv], [pd], scale=-1.0)
                S.act(dst, dst, AF.Ln, [pd], [pd], bias=1.0)
                S.ts("dve", dst, dst, -8.0, ALU.mult, [pd], [pd])
                lbsrc = lbt[:, (d * DEPTH + l) * 16:(d * DEPTH + l + 1) * 16]
                S.copy("dve", pd[:, (2 + d) * 16:(3 + d) * 16], lbsrc, [lbt], [pd])
                S.ts("dve", pd[:, (4 + d) * 16:(5 + d) * 16], lbsrc, -1.0, ALU.mult, [lbt], [pd], s2=1.0, op1=ALU.add)

        def phase1(l):
            with ExitStack() as e1:
                hT = sb("hT", [128, 16, NL], BF16, e1)
                gbc = sb("gbc", [128, D], F32, e1)
                xg = [sb(f"xg{i}", [128, D], F32, e1) for i in range(2)]
                hn = [sb(f"hn{i}", [128, D], BF16, e1) for i in range(2)]
                ss = sb("ss", [128, NG], F32, e1)
                wst = [sb(f"wst{i}", [128, 2048], F32, e1) for i in range(2)]
                wb = [sb(f"wb{i}", [128, 16, 128], BF16, e1) for i in range(2)]
                ost = [sb(f"ost{i}", [128, NL], F32, e1) for i in range(2)]
                hp = sb("hp", [128, 16, 3], F32, e1)
                zst = sb("zst", [128, 8, 48], F32, e1)
                S.dma("sp", gbc[:, :], gains[l:l + 1, :].partition_broadcast(128), [], [gbc])
                xsrc = x_in if l == 0 else XS
                for g in range(NG):
                    xt = xg[g % 2]; h = hn[g % 2]
                    S.dma("sp", xt[:, :], xsrc[g * 128:(g + 1) * 128, :], [bXS[g]], [xt])
                    S.act(h[:, :], xt[:, :], AF.Square, [xt], [h, ss], accum=ss[:, g:g + 1])
                    S.act(ss[:, g:g + 1], ss[:, g:g + 1], AF.Sqrt, [ss], [ss], scale=1.0 / D, bias=EPS)
                    S.op("dve", lambda e: e.reciprocal(out=ss[:, g:g + 1], in_=ss[:, g:g + 1]), [ss], [ss])
                    S.stt(h[:, :], xt[:, :], ss[:, g:g + 1], gbc[:, :], ALU.mult, ALU.mult, [xt, ss, gbc], [h])
                    for kc in range(16):
                        S.tr(pT[:, kc * 128:(kc + 1) * 128], h[:, kc * 128:(kc + 1) * 128], ident[:, :],
                             [h, ident], [pT], signal=(kc == 15))
                    S.copy("act", hT[:, :, g * 128:(g + 1) * 128],
                           pT[:, :].rearrange("p (k t) -> p k t", k=16), [pT], [hT])
                funcs = [AF.Copy, AF.Silu, AF.Silu, AF.Sigmoid, AF.Sigmoid, AF.Copy, AF.Silu, AF.Sigmoid, AF.Sigmoid]
                k = 0
                def wload(c):
                    S.dma("sp", wst[c % 2][:, :], w_in[l, c], [], [wst[c % 2]])
                    S.copy("pool", wb[c % 2][:, :, :], wst[c % 2][:, :].rearrange("p (k c) -> p k c", k=16), [wst[c % 2]], [wb[c % 2]])

                wload(0)
                for c in range(144):
                    s_ = c // 16
                    w1 = wb[c % 2]; o = ost[c % 2]
                    if c + 1 < 144:
                        wload(c + 1)
                    for sub in range(NSUBL):
                        acc = acc4[k % 4]; k += 1
                        for kc in range(16):
                            S.mm(acc[:, 0:SUB], w1[:, kc, :], hT[:, kc, sub * SUB:(sub + 1) * SUB],
                                 kc == 0, kc == 15, [w1, hT], [acc], signal=(kc == 15))
                        S.act(o[:, sub * SUB:(sub + 1) * SUB], acc[:, 0:SUB], funcs[s_], [acc], [o])
                    S.dma("pool", PROJ[c], o[:, :], [o], [bPROJ[c]])
                    if dbg and l == 0:
                        S.dma("pool", dbg_proj[c], o[:, :], [o], [])
                    if s_ == 0:
                        S.copy("dve", hp[:, c, 0:2], o[:, NL - 2:NL], [o], [hp])
                        S.copy("dve", hp[:, c, 2:3], o[:, 0:1], [o], [hp])
                        if c == 15:
                            for j in range(8):
                                S.ts("dve", zst[:, j, :], hp[:, :, :].rearrange("p h t -> p (h t)"), CV(C_OH + j),
                                     ALU.mult, [hp, cv], [zst])
                            S.dma("pool", ZH[:, :], zst[:, :, :].rearrange("p j t -> p (j t)"), [zst], [bZH])
                            S.collective([ZH[:, :]], [RH[:, :]], [bZH], [bRH])
                S.barrier()

        def zipgens(*gens):
            gens = [g for g in gens if g is not None]
            while gens:
                nxt = []
                for g in gens:
                    try:
                        next(g)
                        nxt.append(g)
                    except StopIteration:
                        pass
                gens = nxt

        def phase2(l):
            with ExitStack() as e2:
                xe = sb("xe", [128, NL + 3], F32, e2)
                xc = sb("xc", [128, NL], F32, e2)
                xcb = sb("xcb", [128, NL], BF16, e2)
                hosum = sb("hosum", [128, NL], F32, e2)
                hs = [V(hosum.t[:, sg * TS:(sg + 1) * TS]) for sg in range(NSEG)]
                vT = sb("vT", [64, NL // CH, 128], BF16, e2)
                vb = sb("vb", [128, TS], BF16, e2)
                kdT = [sb(f"kdT{i}", [64, NCH, 128], BF16, e2) for i in range(2)]
                sT = [sb(f"sT{i}", [64, TS], BF16, e2) for i in range(2)]
                tpH = [[sb(f"tp{d}{i}", [128, TS], F32, e2) for i in range(5)] for d in range(2)]
                tbH = [[sb(f"tb{d}{i}", [128, TS], BF16, e2) for i in range(5)] for d in range(2)]
                ldH = [[sb(f"ld{d}{i}", [128, TS], F32, e2) for i in range(4)] for d in range(2)]
                wgs = sb("wgs", [128, 512], F32, e2)
                wgb = sb("wgb", [128, 4, 128], BF16, e2)
                rh = sb("rh", [128, 8, 48], F32, e2)
                Sst = [sb(f"Sst{i}", [128, 128], F32, e2) for i in range(2)]
                Sbf = [[sb(f"Sbf{d}{i}", [128, 128], BF16, e2) for i in range(2)] for d in range(2)]
                smH = [sb(f"sm{d}", [128, 4 * (NCH + 1) + 4], F32, e2) for d in range(2)]
                carH = [sb(f"car{d}", [128, 4], F32, e2) for d in range(2)]
                zeros = sb("zeros", [128, TS], BF16, e2)
                mk = sb("mk", [64, 2, TS], BF16, e2)
                cm = sb("cm", [128, 2, TS], BF16, e2)
                pay, rinit = LS["pay"], LS["rinit"]
                pTh = [V(pT.t[:, d * 1024:(d + 1) * 1024]) for d in range(2)]
                pUb = [pU, pX]
                HW_ = min(512, TS)
                NHALF = TS // HW_
                CPH = HW_ // CH
                S.op("dve", lambda e: e.memset(zeros[:, :], 0.0), [], [zeros])
                for i in range(2):
                    S.dma("sp", tpH[0][i][:, :], cmask_in[i], [], [tpH[0][i]])
                    S.copy("dve", cm[:, i, :], tpH[0][i][:, :], [tpH[0][i]], [cm])
                    S.dma("sp", tpH[1][i][0:64, :], masks_in[i], [], [tpH[1][i]])
                    S.copy("dve", mk[:, i, :], tpH[1][i][0:64, :], [tpH[1][i]], [mk])

                S.dma("sp", rh[:, :, :], RH[:, :].rearrange("p (j t) -> p j t", j=8), [bRH], [rh])
                S.op("dve", lambda e: e.memset(halo[:, :, :], 0.0), [], [halo])
                for j in range(8):
                    rj = rh[:, j, :].rearrange("p (h t) -> p h t", t=3)
                    S.stt(halo[:, :, 0:2], rj[:, :, 0:2], CV(C_SL + j), halo[:, :, 0:2], ALU.mult, ALU.add,
                          [rh, cv, halo], [halo])
                    S.stt(halo[:, :, 2:3], rj[:, :, 2:3], CV(C_SR + j), halo[:, :, 2:3], ALU.mult, ALU.add,
                          [rh, cv, halo], [halo])
                S.op("dve", lambda e: e.memset(pay[:, :], 0.0), [], [pay])

                def convpro(hd):
                    S.dma("sp", xe[:, 2:NL + 2], PROJ[hd], [bPROJ[hd]], [xe])
                    S.dma("sp", wgs[:, :], w_g[l, hd], [], [wgs])
                    yield
                    S.copy("act", wgb[:, :, :], wgs[:, :].rearrange("p (g j) -> p g j", g=4), [wgs], [wgb])
                    S.copy("dve", xe[:, 0:2], halo[:, hd, 0:2], [halo], [xe])
                    S.copy("dve", xe[:, NL + 2:NL + 3], halo[:, hd, 2:3], [halo], [xe])
                    yield
                    S.act(xc[:, :], xe[:, 0:NL], AF.Identity, [xe, pv], [xc], bias=P(4, hd), scale=P(0, hd))
                    yield
                    for j in range(1, 4):
                        S.stt(xc[:, :], xe[:, j:j + NL], P(j, hd), xc[:, :], ALU.mult, ALU.add, [xe, pv, xc], [xc])
                        yield
                    fix = [(TS - 1, 3, TS), (TS, 0, TS - 2), (TS, 1, TS - 1), (TS + 1, 0, TS - 1)]
                    for (tcol, tap, scol) in fix:
                        S.ts("dve", tiny[:, 0:1], xe[:, scol + 2:scol + 3], P(tap, hd), ALU.mult, [xe, pv], [tiny])
                        S.stt(xc[:, tcol:tcol + 1], tiny[:, 0:1], CV(C_NOM), xc[:, tcol:tcol + 1], ALU.mult, ALU.add,
                              [tiny, cv, xc], [xc])
                        yield
                    S.copy("act", xcb[:, :], xc[:, :], [xc], [xcb])
                    yield

                def vpro(hd):
                    for sg in range(NSEG):
                        sl = slice(sg * TS, (sg + 1) * TS)
                        vv = ldH[sg % 2][3]
                        S.dma("sp", vv[:, :], PROJ[5 * 16 + hd][:, sl], [bPROJ[5 * 16 + hd]], [vv])
                        yield
                        S.copy("act", vb[:, :], vv[:, :], [vv], [vb])
                        yield
                        for hf in range(NHALF):
                            for c in range(CPH):
                                cc = hf * CPH + c
                                S.tr(pTh[hf][0:64, c * 128:(c + 1) * 128], vb[:, cc * CH:(cc + 1) * CH], ident[:, :],
                                     [vb, ident], [pTh[hf]], signal=(c == CPH - 1))
                            yield
                            S.copy("act", vT[:, sg * NCH + hf * CPH:sg * NCH + (hf + 1) * CPH, :],
                                   pTh[hf][0:64, 0:CPH * 128].rearrange("p (c v) -> p c v", c=CPH), [pTh[hf]], [vT])
                            yield

                def rg_chain(hd, d):
                    t0, t1, t2 = tpH[d][0], tpH[d][1], tpH[d][2]
                    car = carH[d]
                    S.op("dve", lambda e: e.memset(car[:, 0:1], 0.0), [], [car])
                    S.op("dve", lambda e: e.memset(car[:, 1:2], 1.0), [], [car])
                    yield
                    segs = list(range(NSEG)) if d == 0 else list(range(NSEG - 1, -1, -1))
                    for si, sg in enumerate(segs):
                        sl = slice(sg * TS, (sg + 1) * TS)
                        A_ = tpH[d][3 + (si % 2)]
                        for sub in range(NSUBS):
                            cs = slice(sg * TS + sub * SUB, sg * TS + (sub + 1) * SUB)
                            co = slice(sub * SUB, (sub + 1) * SUB)
                            pr, pi_ = pA[d], pB[d]
                            S.mm(pr[:, 0:SUB], wgb[:, 2 * d, :], xcb[:, cs], True, True, [wgb, xcb], [pr])
                            S.mm(pi_[:, 0:SUB], wgb[:, 2 * d + 1, :], xcb[:, cs], True, True, [wgb, xcb], [pi_])
                            yield
                            S.act(t0[:, co], pr[:, 0:SUB], AF.Sigmoid, [pr, pv], [t0], bias=P(5 + 3 * d, hd))
                            yield
                            S.act(t1[:, co], pi_[:, 0:SUB], AF.Sigmoid, [pi_, pv], [t1], bias=P(6 + 3 * d, hd))
                            yield
                        S.act(t2[:, :], t0[:, :], AF.Exp, [t0, pd], [t2], scale=PD(d, hd))
                        yield
                        S.tt("dve", t1[:, :], t1[:, :], xc[:, sl], ALU.mult, [t1, xc], [t1])
                        yield
                        S.act(t0[:, :], t2[:, :], AF.Square, [t2], [t0])
                        yield
                        S.act(t0[:, :], t0[:, :], AF.Sqrt, [t0], [t0], scale=-1.0, bias=1.0)
                        yield
                        S.tt("dve", t1[:, :], t1[:, :], t0[:, :], ALU.mult, [t1, t0], [t1])
                        yield
                        if (d == 0 and sg == 1) or (d == 1 and sg == 0):
                            S.ts("dve", car[:, 0:2], car[:, 0:2], CV(C_RS), ALU.mult, [car, cv], [car])
                            yield
                        first = ("h", hd, sg) not in wrote
                        wrote.add(("h", hd, sg))
                        hb_ = hs[sg] if first else t0
                        rv = (lambda ap: ap) if d == 0 else (lambda ap: ap[:, ::-1])
                        S.scan(rv(hb_[:, :]), rv(t2[:, :]), rv(t1[:, :]), car[:, 0:1], [t2, t1, car], [hb_])
                        yield
                        S.scan(rv(A_[:, :]), rv(t2[:, :]), zeros[:, :], car[:, 1:2], [t2, zeros, car], [A_])
                        yield
                        lc = slice(TS - 1, TS) if d == 0 else slice(0, 1)
                        S.copy("dve", car[:, 0:1], hb_[:, lc], [hb_], [car])
                        S.copy("dve", car[:, 1:2], A_[:, lc], [A_], [car])
                        yield
                        if not first:
                            S.tt("dve", hs[sg][:, :], hs[sg][:, :], t0[:, :], ALU.add, [t0, hs[sg]], [hs[sg]])
                            yield
                        if d == 0:
                            S.dma("pool", AFd[hd][:, sl], A_[:, :], [A_], [bAF[hd]])
                        else:
                            S.dma("pool", ABd[hd][:, sl], A_[:, :], [A_], [bAB[hd]])
                        yield
                    oh, oa = (O_HF, O_AF) if d == 0 else (O_HB, O_AB)
                    S.copy("dve", pay[:, oh + hd:oh + hd + 1], car[:, 0:1], [car], [pay])
                    S.copy("dve", pay[:, oa + hd:oa + hd + 1], car[:, 1:2], [car], [pay])
                    yield

                def hg_chain(hd, d):
                    t = tpH[d]; tbc = tbH[d]; ldc = ldH[d]
                    St = Sst[d]; Sb2 = Sbf[d]; sm_ = smH[d]; car = carH[d]
                    S.op("dve", lambda e: e.memset(St[:, :], 0.0), [], [St])
                    S.op("dve", lambda e: e.memset(car[:, 2:3], 1.0), [], [car])
                    sbi = 0
                    yield
                    segs = list(range(NSEG)) if d == 0 else list(range(NSEG - 1, -1, -1))
                    r3 = lambda ap: ap.rearrange("p (c t) -> p c t", t=CH)

                    qbuf = [ldc[0], ldc[1]]

                    def do_loads(si):
                        sg = segs[si]
                        sl = slice(sg * TS, (sg + 1) * TS)
                        S.dma("sp", qbuf[si % 2][:, :], PROJ[2 * 16 + hd][:, sl], [bPROJ[2 * 16 + hd]], [qbuf[si % 2]])
                        S.dma("sp", ldc[2][:, :], PROJ[(3 + d) * 16 + hd][:, sl], [bPROJ[(3 + d) * 16 + hd]], [ldc[2]])

                    do_loads(0)
                    yield
                    for si, sg in enumerate(segs):
                        sl = slice(sg * TS, (sg + 1) * TS)
                        qs = qbuf[si % 2]; sf = ldc[2]
                        f_, lf, k_, bc, E2 = t[0], t[1], t[2], t[3], t[4]
                        D_, E1 = t[0], t[1]
                        qt, qd_, kt, kd_, qG = tbc[0], tbc[1], tbc[2], tbc[3], tbc[4]
                        S.act(f_[:, :], sf[:, :], AF.Identity, [sf, pd], [f_], scale=PD(4 + d, hd), bias=PD(2 + d, hd))
                        yield
                        if si + 1 < NSEG:
                            do_loads(si + 1)
                        S.act(lf[:, :], f_[:, :], AF.Ln, [f_], [lf])
                        yield
                        S.act(k_[:, :], f_[:, :], AF.Identity, [f_], [k_], scale=-1.0, bias=1.0)
                        yield
                        bc3 = r3(bc[:, :])
                        if d == 0:
                            S.scan(bc[:, :], cm[:, 0, :], lf[:, :], 0.0, [cm, lf], [bc])
                            bl = bc3[:, :, CH - 1:CH]
                        else:
                            S.scan(bc[:, ::-1], cm[:, 1, ::-1], lf[:, ::-1], 0.0, [cm, lf], [bc])
                            bl = bc3[:, :, 0:1]
                        yield
                        blb = bl.to_broadcast([128, NCH, CH])
                        Eh = sm_[:, 0:NCH]; E5 = sm_[:, NCH:2 * NCH]
                        Gx = sm_[:, 2 * NCH:3 * NCH + 1]
                        c1 = lambda ap: ap.rearrange("p (c o) -> p c o", o=1)
                        S.act(c1(Eh), bl, AF.Exp, [bc], [sm_], scale=0.5)
                        yield
                        S.stt(r3(D_[:, :]), blb, -0.5, bc3, ALU.mult, ALU.add, [bc], [D_])
                        yield
                        S.act(E2[:, :], D_[:, :], AF.Exp, [D_], [E2], scale=-1.0)
                        yield
                        S.act(E1[:, :], D_[:, :], AF.Exp, [D_], [E1])
                        yield
                        S.tt("dve", E5, Eh, Eh, ALU.mult, [sm_], [sm_])
                        Ehb = c1(Eh).to_broadcast([128, NCH, CH])
                        S.tt("dve", kt[:, :], k_[:, :], E2[:, :], ALU.mult, [k_, E2], [kt])
                        yield
                        S.tt("dve", qt[:, :], qs[:, :], E1[:, :], ALU.mult, [qs, E1], [qt])
                        yield
                        S.tt("dve", r3(kd_[:, :]), r3(kt[:, :]), Ehb, ALU.mult, [kt, sm_], [kd_])
                        yield
                        if (d == 0 and sg == 1) or (d == 1 and sg == 0):
                            S.ts("dve", car[:, 2:3], car[:, 2:3], CV(C_RS), ALU.mult, [car, cv], [car])
                            S.ts("dve", St[:, :], St[:, :], CV(C_RS), ALU.mult, [St, cv], [St])
                            yield
                        if d == 0:
                            S.copy("dve", Gx[:, 0:1], car[:, 2:3], [car], [sm_])
                            S.scan(Gx[:, 1:NCH + 1], E5, zeros[:, 0:NCH], car[:, 2:3], [sm_, zeros, car], [sm_])
                            S.copy("dve", car[:, 2:3], Gx[:, NCH:NCH + 1], [sm_], [car])
                            Gex = Gx[:, 0:NCH]
                        else:
                            S.copy("dve", Gx[:, NCH:NCH + 1], car[:, 2:3], [car], [sm_])
                            S.scan(Gx[:, 0:NCH][:, ::-1], E5[:, ::-1], zeros[:, 0:NCH], car[:, 2:3], [sm_, zeros, car], [sm_])
                            S.copy("dve", car[:, 2:3], Gx[:, 0:1], [sm_], [car])
                            Gex = Gx[:, 1:NCH + 1]
                        yield
                        EG = sm_[:, 3 * NCH + 1:4 * NCH + 1]
                        S.tt("dve", EG, Eh, Gex, ALU.mult, [sm_], [sm_])
                        S.tt("dve", r3(qG[:, :]), r3(qt[:, :]), c1(EG).to_broadcast([128, NCH, CH]), ALU.mult, [qt, sm_], [qG])
                        yield
                        S.dma("pool", (QGF if d == 0 else QGB)[hd][:, sl], qG[:, :], [qG], [(bQGF if d == 0 else bQGB)[hd]])
                        for hf in range(NHALF):
                            for c in range(CPH):
                                cc = hf * CPH + c
                                S.tr(pTh[d][0:64, c * 128:(c + 1) * 128], kd_[:, cc * CH:(cc + 1) * CH], ident[:, :],
                                     [kd_, ident], [pTh[d]], signal=(c == CPH - 1))
                            yield
                            S.copy("act", kdT[d][:, hf * CPH:(hf + 1) * CPH, :],
                                   pTh[d][0:64, 0:CPH * 128].rearrange("p (c v) -> p c v", c=CPH), [pTh[d]], [kdT[d]])
                            yield
                        for hf in range(NHALF):
                            for c in range(CPH):
                                cc = hf * CPH + c
                                S.mm(pA[d][0:64, c * CH:(c + 1) * CH], kt[:, cc * CH:(cc + 1) * CH], qt[:, cc * CH:(cc + 1) * CH],
                                     True, True, [kt, qt], [pA[d]], signal=(c == CPH - 1))
                            yield
                            S.tt("dve", sT[d][:, hf * HW_:(hf + 1) * HW_], pA[d][0:64, 0:HW_], mk[:, d, hf * HW_:(hf + 1) * HW_],
                                 ALU.mult, [pA[d], mk], [sT[d]])
                            yield
                        corder = list(range(NCH)) if d == 0 else list(range(NCH - 1, -1, -1))
                        c0 = corder[0]
                        S.act(Sb2[sbi % 2][:, :], St[:, :], AF.Identity, [St, sm_], [Sb2[sbi % 2]], scale=Eh[:, c0:c0 + 1])
                        yield

                        def evac(hf):
                            osl = hs[sg][:, hf * HW_:(hf + 1) * HW_]
                            first = ("o", hd, sg, hf) not in wrote
                            wrote.add(("o", hd, sg, hf))
                            if first:
                                S.copy("act", osl, pB[d][:, 0:HW_], [pB[d]], [hs[sg]])
                            else:
                                S.tt("dve", osl, osl, pB[d][:, 0:HW_], ALU.add, [pB[d], hs[sg]], [hs[sg]])

                        for ci, c in enumerate(corder):
                            hf = c // CPH
                            off = (c % CPH) * CH
                            vTc = vT[:, sg * NCH + c, :]
                            pub = pUb[d]
                            pu = pub[:, (ci % 4) * 128:(ci % 4 + 1) * 128] if d == 1 else pub[:, ci % 4, :]
                            S.mm(pB[d][:, off:off + CH], vTc, sT[d][:, c * CH:(c + 1) * CH], True, False, [vT, sT[d]], [pB[d]], signal=False)
                            S.mm(pB[d][:, off:off + CH], Sb2[sbi % 2][:, :], qt[:, c * CH:(c + 1) * CH], False, True,
                                 [Sb2[sbi % 2], qt], [pB[d]])
                            S.mm(pu, kdT[d][:, c, :], vTc, True, True, [kdT[d], vT], [pub])
                            yield
                            S.stt(St[:, :], St[:, :], E5[:, c:c + 1], pu, ALU.mult, ALU.add, [St, sm_, pub], [St])
                            yield
                            sbi += 1
                            if ci + 1 < NCH:
                                cn = corder[ci + 1]
                                S.act(Sb2[sbi % 2][:, :], St[:, :], AF.Identity, [St, sm_], [Sb2[sbi % 2]], scale=Eh[:, cn:cn + 1])
                                yield
                            if (ci + 1) % CPH == 0:
                                evac(hf)
                                yield
                    od, os_ = (O_DF, O_SF) if d == 0 else (O_DB, O_SB)
                    S.copy("dve", pay[:, od + hd:od + hd + 1], car[:, 2:3], [car], [pay])
                    S.copy("act", pay[:, os_ + hd * 128:os_ + (hd + 1) * 128], St[:, :], [St], [pay])
                    yield

                wrote = set()
                zipgens(convpro(0))
                for hd in range(NH):
                    zipgens(rg_chain(hd, 0), rg_chain(hd, 1), vpro(hd))
                    S.dma("pool", HS[hd], hosum[:, :], hs, [bHS[hd]])
                    zipgens(hg_chain(hd, 0), hg_chain(hd, 1), convpro(hd + 1) if hd + 1 < NH else None)
                    S.dma("pool", OS[hd], hosum[:, :], hs, [bOS[hd]])
                S.barrier()

        def exchange():
            with ExitStack() as e3:
                pay, sinit, rinit = LS["pay"], LS["sinit"], LS["rinit"]
                zs = [sb(f"zs{i}", [128, PAY], F32, e3) for i in range(2)]
                rj = [sb(f"rj{i}", [128, PAY], F32, e3) for i in range(2)]
                Sc = sb("Sc", [128, 2, 2048], F32, e3)
                rc = sb("rc", [128, 2, 16], F32, e3)
                dg = sb("dg", [128, 2, 16], F32, e3)
                tt_ = sb("tt_", [128, 2048], F32, e3)
                for j in range(8):
                    z = zs[j % 2]
                    if j % 2:
                        S.ts("dve", z[:, :], pay[:, :], CV(C_OH + j), ALU.mult, [pay, cv], [z])
                    else:
                        S.act(z[:, :], pay[:, :], AF.Identity, [pay, cv], [z], scale=CV(C_OH + j))
                    S.dma("pool", Z[j * 128:(j + 1) * 128, :], z[:, :], [z], [bZ])
                S.collective([Z[:, :]], [R[:, :]], [bZ], [bR])
                S.op("dve", lambda e: e.memset(Sc[:, :, :], 0.0), [], [Sc])
                S.op("dve", lambda e: e.memset(rc[:, :, :], 0.0), [], [rc])
                for d in range(2):
                    order = range(8) if d == 0 else range(7, -1, -1)
                    oh, oa, od, os_ = (O_HF, O_AF, O_DF, O_SF) if d == 0 else (O_HB, O_AB, O_DB, O_SB)
                    for n_, j in enumerate(order):
                        r_ = rj[n_ % 2]
                        g = CV((C_GF if d == 0 else C_GB) + j)
                        S.dma("sp", r_[:, :], R[j * 128:(j + 1) * 128, :], [bR], [r_])
                        S.ts("dve", dg[:, 0, :], r_[:, oa:oa + 16], -1.0, ALU.add, [r_], [dg])
                        S.ts("dve", dg[:, 0, :], dg[:, 0, :], g, ALU.mult, [dg, cv], [dg], s2=1.0, op1=ALU.add)
                        S.tt("dve", rc[:, d, :], rc[:, d, :], dg[:, 0, :], ALU.mult, [rc, dg], [rc])
                        S.stt(rc[:, d, :], r_[:, oh:oh + 16], g, rc[:, d, :], ALU.mult, ALU.add, [r_, cv, rc], [rc])
                        S.ts("dve", dg[:, 1, :], r_[:, od:od + 16], -1.0, ALU.add, [r_], [dg])
                        S.ts("dve", dg[:, 1, :], dg[:, 1, :], g, ALU.mult, [dg, cv], [dg], s2=1.0, op1=ALU.add)
                        s3 = Sc[:, d, :].rearrange("p (h v) -> p h v", h=16)
                        S.tt("dve", tt_[:, :].rearrange("p (h v) -> p h v", h=16), s3,
                             dg[:, 1, :].rearrange("p (h o) -> p h o", o=1).to_broadcast([128, 16, 128]), ALU.mult, [Sc, dg], [tt_])
                        S.stt(Sc[:, d, :], r_[:, os_:os_ + 2048], g, tt_[:, :], ALU.mult, ALU.add, [r_, cv, tt_], [Sc])
                S.copy("dve", sinit[:, :, :], Sc[:, :, :], [Sc], [sinit])
                S.copy("dve", rinit[:, :, :], rc[:, :, :], [rc], [rinit])
                S.barrier()

        def phase34(l, last):
            with ExitStack() as e4:
                pay, sinit, rinit = LS["pay"], LS["sinit"], LS["rinit"]
                urT = sb("urT", [128, 16, SUB], BF16, e4)
                uhT = sb("uhT", [128, 16, SUB], BF16, e4)
                mT = sb("mT", [128, 16, SUB], BF16, e4)
                lfc = [[sb(f"lf{c}{i}", [128, SUB], F32, e4) for i in range(6)] for c in range(2)]
                lf_ = [sb(f"lfp{i}", [128, SUB], F32, e4) for i in range(2)]
                lbc = [[sb(f"lb{c}{i}", [128, SUB], BF16, e4) for i in range(2)] for c in range(2)]
                print("P34 free", nc.sbuf_bytes_remaining) if False else None
                tA = [[sb(f"t{c}{i}", [128, SUB], F32, e4) for i in range(3)] for c in range(2)]
                t_ = tA[0]
                o2 = [sb(f"o2{c}", [128, SUB], BF16, e4) for c in range(2)]
                wst = [sb(f"wst{i}", [128, 2048], F32, e4) for i in range(2)]
                wb = [sb(f"wb{i}", [128, 16, 128], BF16, e4) for i in range(2)]
                wo = [sb(f"wo{i}", [128, 16, 256], BF16, e4) for i in range(2)]
                xr_ = [sb(f"xr{i}", [128, D], F32, e4) for i in range(GPS)]
                if last:
                    gbc = sb("gbc", [128, D], F32, e4)
                ss = sb("ss", [128, 4], F32, e4)
                li = [0]; bi_ = [0]; wi = [0]; woi = [0]

                def loadf(src, b):
                    t = lf_[li[0] % len(lf_)]; li[0] += 1
                    S.dma("sp", t[:, :], src, [b], [t]); return t

                def loadb(src, b):
                    t = lb_[bi_[0] % len(lb_)]; bi_[0] += 1
                    S.dma("sp", t[:, :], src, [b], [t]); return t

                def loadw(src):
                    i = wi[0] % 2; wi[0] += 1
                    S.dma("sp", wst[i][:, :], src, [], [wst[i]])
                    S.copy("act", wb[i][:, :, :], wst[i][:, :].rearrange("p (k c) -> p k c", k=16), [wst[i]], [wb[i]])
                    return wb[i]

                if last:
                    S.dma("sp", gbc[:, :], gains[DEPTH:DEPTH + 1, :].partition_broadcast(128), [], [gbc])
                xsrc = x_in if l == 0 else XS
                for st in range(NSUBL):
                    cs = slice(st * SUB, (st + 1) * SUB)
                    def p3_chain(hds, ci):
                        t0, t1, t3 = tA[ci]
                        o2c = o2[ci]
                        pxc = pX if ci == 0 else pU
                        pxa = pX[:, 0:SUB] if ci == 0 else pU[:, :, :].rearrange("p a b -> p (a b)")[:, 0:SUB]
                        for hd in hds:
                            def ldt(t, src, b):
                                S.dma("sp", t[:, :], src, [b], [t]); return t
                            L = lfc[ci]
                            hs = ldt(L[0], HS[hd][:, cs], bHS[hd]); af = ldt(L[1], AFd[hd][:, cs], bAF[hd])
                            ab = ldt(L[2], ABd[hd][:, cs], bAB[hd]); sg_ = ldt(L[3], PROJ[16 + hd][:, cs], bPROJ[16 + hd])
                            os_ = ldt(L[4], OS[hd][:, cs], bOS[hd]); gg = ldt(L[5], PROJ[6 * 16 + hd][:, cs], bPROJ[6 * 16 + hd])
                            qf = ldt(lbc[ci][0], QGF[hd][:, cs], bQGF[hd]); qb = ldt(lbc[ci][1], QGB[hd][:, cs], bQGB[hd])
                            yield
                            S.mm(pxa, sinit[:, 0, hd * 128:(hd + 1) * 128], qf[:, :], True, False, [sinit, qf], [pxc], signal=False)
                            S.mm(pxa, sinit[:, 1, hd * 128:(hd + 1) * 128], qb[:, :], False, True, [sinit, qb], [pxc])
                            yield
                            S.stt(t0[:, :], af[:, :], rinit[:, 0, hd:hd + 1], hs[:, :], ALU.mult, ALU.add, [af, rinit, hs], [t0])
                            yield
                            S.stt(t0[:, :], ab[:, :], rinit[:, 1, hd:hd + 1], t0[:, :], ALU.mult, ALU.add, [ab, rinit, t0], [t0])
                            yield
                            S.tt("dve", urT[:, hd, :], t0[:, :], sg_[:, :], ALU.mult, [t0, sg_], [urT])
                            yield
                            if dbg and l == 0:
                                S.tt("dve", t3[:, :], t0[:, :], sg_[:, :], ALU.mult, [t0, sg_], [t3])
                                S.dma("pool", dbg_u[0, hd][:, cs], t3[:, :], [t3], [])
                            S.tt("dve", t1[:, :], os_[:, :], pxa, ALU.add, [os_, pxc], [t1])
                            yield
                            S.act(o2c[:, :], t1[:, :], AF.Square, [t1], [o2c])
                            yield
                            S.mm(pxa, ones_bf[:, :], o2c[:, :], True, True, [ones_bf, o2c], [pxc])
                            yield
                            S.act(t3[:, :], pxa, AF.Sqrt, [pxc], [t3], scale=1.0 / 128, bias=EPS)
                            yield
                            S.op("dve", lambda e: e.reciprocal(out=t3[:, :], in_=t3[:, :]), [t3], [t3])
                            yield
                            S.tt("dve", t1[:, :], t1[:, :], t3[:, :], ALU.mult, [t1, t3], [t1])
                            yield
                            S.stt(uhT[:, hd, :], t1[:, :], P(11, hd), gg[:, :], ALU.mult, ALU.mult, [t1, pv, gg], [uhT])
                            yield
                            if dbg and l == 0:
                                S.stt(t3[:, :], t1[:, :], P(11, hd), gg[:, :], ALU.mult, ALU.mult, [t1, pv, gg], [t3])
                                S.dma("pool", dbg_u[1, hd][:, cs], t3[:, :], [t3], [])

                    zipgens(p3_chain(list(range(0, NH, 2)), 0), p3_chain(list(range(1, NH, 2)), 1))
                    for g in range(GPS):
                        gi = st * GPS + g
                        S.dma("sp", xr_[g][:, :], xsrc[gi * 128:(gi + 1) * 128, :], [bXS[gi]], [xr_[g]])
                    jobs = []
                    for cc in range(16):
                        jobs.append(("dr", cc)); jobs.append(("dh", cc))
                    for dgp in range(8):
                        jobs.append(("wo", dgp))
                    wbs = {}

                    def issue(j):
                        kind, idx = jobs[j]
                        if kind in ("dr", "dh"):
                            wbs[j] = loadw((w_dr if kind == "dr" else w_dh)[l, idx])
                        else:
                            w_o = wo[woi[0] % 2]; woi[0] += 1
                            dg_, h_ = idx // 2, idx % 2
                            src3 = w_out[l, dg_].rearrange("p (k c) -> p k c", k=16)
                            for q in range(2):
                                i = wi[0] % 2; wi[0] += 1
                                S.dma("sp", wst[i][:, :].rearrange("p (k c) -> p k c", k=8), src3[:, q * 8:(q + 1) * 8, h_ * 256:(h_ + 1) * 256], [], [wst[i]])
                                S.copy("act", w_o[:, q * 8:(q + 1) * 8, :], wst[i][:, :].rearrange("p (k c) -> p k c", k=8), [wst[i]], [w_o])
                            wbs[j] = w_o

                    issue(0)
                    k = 0
                    for j, (kind, idx) in enumerate(jobs):
                        if j + 1 < len(jobs):
                            issue(j + 1)
                        w_ = wbs.pop(j)
                        if kind == "dr":
                            cc = idx
                            for kc in range(16):
                                S.mm(pA[cc % 2][:, 0:SUB], w_[:, kc, :], urT[:, kc, :], kc == 0, kc == 15, [w_, urT], [pA[cc % 2]], signal=(kc == 15))
                        elif kind == "dh":
                            cc = idx
                            for kc in range(16):
                                S.mm(pB[cc % 2][:, 0:SUB], w_[:, kc, :], uhT[:, kc, :], kc == 0, kc == 15, [w_, uhT], [pB[cc % 2]], signal=(kc == 15))
                            smr = loadf(PROJ[7 * 16 + cc][:, cs], bPROJ[7 * 16 + cc])
                            smh = loadf(PROJ[8 * 16 + cc][:, cs], bPROJ[8 * 16 + cc])
                            S.tt("dve", t_[0][:, :], pA[cc % 2][:, 0:SUB], smr[:, :], ALU.mult, [pA[cc % 2], smr], [t_[0]])
                            S.tt("dve", t_[1][:, :], pB[cc % 2][:, 0:SUB], smh[:, :], ALU.mult, [pB[cc % 2], smh], [t_[1]])
                            S.tt("dve", mT[:, cc, :], t_[0][:, :], t_[1][:, :], ALU.add, [t_[0], t_[1]], [mT])
                        else:
                            dgp = idx
                            for g in range(GPS):
                                acc = acc4[k % 4]; k += 1
                                for kc in range(16):
                                    S.mm(acc[:, 0:256], mT[:, kc, g * 128:(g + 1) * 128], w_[:, kc, :], kc == 0, kc == 15, [mT, w_], [acc], signal=(kc == 15))
                                S.tt("dve", xr_[g][:, dgp * 256:(dgp + 1) * 256], xr_[g][:, dgp * 256:(dgp + 1) * 256], acc[:, 0:256], ALU.add,
                                     [acc, xr_[g]], [xr_[g]])
                    for g in range(GPS):
                        gi = st * GPS + g
                        if not last:
                            S.dma("pool", XS[gi * 128:(gi + 1) * 128, :], xr_[g][:, :], [xr_[g]], [bXS[gi]])
                        else:
                            if dbg:
                                S.dma("pool", dbg_x[gi * 128:(gi + 1) * 128, :], xr_[g][:, :], [xr_[g]], [])
                            S.act(urT[:, 0:4, :].rearrange("p a b -> p (a b)") if SUB == 512 else urT[:, :, :].rearrange("p a b -> p (a b)"),
                                  xr_[g][:, :], AF.Square, [xr_[g]], [urT, ss], accum=ss[:, g:g + 1])
                            S.act(ss[:, g:g + 1], ss[:, g:g + 1], AF.Sqrt, [ss], [ss], scale=1.0 / D, bias=EPS)
                            S.op("dve", lambda e: e.reciprocal(out=ss[:, g:g + 1], in_=ss[:, g:g + 1]), [ss], [ss])
                            S.stt(xr_[g][:, :], xr_[g][:, :], ss[:, g:g + 1], gbc[:, :], ALU.mult, ALU.mult, [xr_[g], ss, gbc], [xr_[g]])
                            S.dma("pool", y_out[gi * 128:(gi + 1) * 128, :], xr_[g][:, :], [xr_[g]], [bXS[gi]])
                S.barrier()

        for l in range(DEPTH):
            layer_params(l)
            phase1(l)
            with ExitStack() as el:
                LS["pay"] = sb("pay", [128, PAY], F32, el)
                LS["rinit"] = sb("rinit", [128, 2, 16], F32, el)
                phase2(l)
                LS["sinit"] = sb("sinit", [128, 2, 2048], BF16, el)
                exchange()
                phase34(l, l == DEPTH - 1)
        S.barrier()
        nc._ninst = S.n
    return nc


def host_layout(inputs, TS, DEPTH):
    f = lambda a: np.ascontiguousarray(np.asarray(a, dtype=np.float32))
    NL = NSEG * TS
    xp = f(inputs["x_prompt"])[0]
    xs = f(inputs["x_sample"])[0]
    stream = np.concatenate([xp, xs], axis=0)
    assert stream.shape[0] == NCORE * NL
    w_in = f(inputs["w_in"])[:DEPTH]
    w_in_l = np.ascontiguousarray(w_in.reshape(DEPTH, 16, 128, 144, 128).transpose(0, 3, 2, 1, 4)).reshape(DEPTH, 144, 128, 2048)
    lay = lambda w: np.ascontiguousarray(f(w)[:DEPTH].reshape(DEPTH, 16, 128, 16, 128).transpose(0, 3, 2, 1, 4)).reshape(DEPTH, 16, 128, 2048)
    w_dr = lay(inputs["w_down_r"]); w_dh = lay(inputs["w_down_h"])
    w_out = np.ascontiguousarray(f(inputs["w_out"])[:DEPTH].reshape(DEPTH, 16, 128, 4, 512).transpose(0, 3, 2, 1, 4)).reshape(DEPTH, 4, 128, 16 * 512)
    wa = f(inputs["rg_wa"])[:DEPTH]; wx = f(inputs["rg_wx"])[:DEPTH]
    w_g = np.stack([wa[:, 0], wx[:, 0], wa[:, 1], wx[:, 1]], axis=1)
    w_g = np.ascontiguousarray(w_g.transpose(0, 2, 3, 1, 4)).reshape(DEPTH, 16, 128, 512)
    gains = np.concatenate([f(inputs["norm_gain"])[:DEPTH], f(inputs["final_gain"])[None]], axis=0)
    ph = lambda v: f(v).reshape(-1, 16, 128)
    cw = f(inputs["conv_w"])[:DEPTH]
    plist = []
    for l in range(DEPTH):
        rows = [cw[l, j] for j in range(4)] + [f(inputs["conv_b"])[l]]
        for d in range(2):
            rows += [f(inputs["rg_ba"])[l, d], f(inputs["rg_bx"])[l, d], f(inputs["rg_lambda"])[l, d]]
        rows += [f(inputs["hg_norm_gain"])[l]]
        rows += [np.zeros(D, np.float32)] * 2
        a = np.stack(rows, 0).reshape(14, 16, 128).transpose(2, 0, 1).reshape(128, 14 * 16)
        plist.append(a)
    pvec = np.ascontiguousarray(np.stack(plist, 0))
    hgl = np.ascontiguousarray(f(inputs["hg_lower"]).reshape(2, DEPTH, 16, 128).transpose(3, 0, 1, 2)).reshape(128, 2 * DEPTH * 16)
    ident = np.eye(128, dtype=np.float32)
    s_ = np.arange(64)[:, None]; t_ = np.arange(TS)[None, :] % 64
    masks = np.stack([(s_ <= t_), (s_ >= t_)], 0).astype(np.float32)
    cm = np.ones((2, 128, TS), np.float32)
    cm[0, :, 0::64] = 0.0
    cm[1, :, 63::64] = 0.0
    bcore = (16 * TS) // NL
    assert (16 * TS) % NL == TS
    maps = []
    for c in range(NCORE):
        cvv = np.zeros(NCV, np.float32)
        cvv[0 + c] = 1.0
        cvv[8:8 + c] = 1.0
        cvv[16 + c + 1:24] = 1.0
        if c > 0:
            cvv[24 + c - 1] = 1.0
        if c < 7:
            cvv[32 + c + 1] = 1.0
        rs = 0.0 if c == bcore else 1.0
        cvv[40] = rs
        cvv[41] = -(1.0 - rs)
        maps.append({
            "x": np.ascontiguousarray(stream[c * NL:(c + 1) * NL]),
            "w_in": w_in_l, "w_dr": w_dr, "w_dh": w_dh, "w_out": w_out, "w_g": w_g, "gains": gains,
            "pvec": pvec, "hgl": hgl, "cvec": np.ascontiguousarray(np.tile(cvv[None], (128, 1))),
            "ident": ident, "masks": masks, "cmask": cm,
        })
    return maps


_CACHE = {}


def kernel(**inputs):
    TS, DEPTH = 1024, 4
    if "nc" not in _CACHE:
        _CACHE["nc"] = build(TS, DEPTH)
    nc = _CACHE["nc"]
    maps = host_layout(inputs, TS, DEPTH)
    res = run_bass_kernel_spmd(nc, maps, core_ids=list(range(NCORE)))
    y = np.concatenate([np.asarray(r["y"]) for r in res.results], axis=0)
    n_p = inputs["x_prompt"].shape[1]
    return (np.ascontiguousarray(y[:n_p])[None].astype(np.float32),
            np.ascontiguousarray(y[n_p:])[None].astype(np.float32))
```
